# Optimizing a Trainium2 kernel written in Bass

```python
import math
import jax, jax.numpy as jnp
from jax import lax
import numpy as np

D_MODEL = 1024
BATCH = 8
SEQ = 4096
DEPTH = 2

N_MIXERS = 2
N_META = 16
POOL_WINDOWS = (2, 4, 8, 16)
N_POOL_GROUPS = len(POOL_WINDOWS)
POOL_GC = D_MODEL // N_POOL_GROUPS
HEAD_DIM = 64
N_HEADS = D_MODEL // (2 * HEAD_DIM)
V_DIM = 2 * HEAD_DIM
Q_BLOCK = 128
N_BUCKETS = 32
MAX_DISTANCE = 128
D_FF = ((8 * D_MODEL + 3 * 256 - 1) // (3 * 256)) * 256
N_POOL_LAYERS = (DEPTH + 1) // 2
N_ATTN_LAYERS = DEPTH // 2
RMS_EPS = 1e-6
NEG_INF = -1e30

kernel_name = "hybrid_pool_diffattn_swiglu"


def rms_norm(x, g):
    xf = x.astype(jnp.float32)
    y = xf * lax.rsqrt(jnp.mean(xf * xf, axis=-1, keepdims=True) + RMS_EPS)
    return (y * g.astype(jnp.float32)).astype(x.dtype)


def pool_mixer(h, w_groups, scale):
    B, L, D = h.shape
    hf = h.astype(jnp.float32)
    c0 = jnp.concatenate([jnp.zeros((B, 1, D), jnp.float32), jnp.cumsum(hf, axis=1)], axis=1)
    t = jnp.arange(L)
    outs = []
    for g, w in enumerate(POOL_WINDOWS):
        sl = slice(g * POOL_GC, (g + 1) * POOL_GC)
        cg = c0[:, :, sl]
        lead = jnp.concatenate([jnp.zeros((B, w - 1, POOL_GC), jnp.float32), cg[:, : L + 1 - w]], axis=1)
        cnt = jnp.minimum(t + 1, w).astype(jnp.float32)[None, :, None]
        pooled = (cg[:, 1:] - lead) / cnt - hf[:, :, sl]
        outs.append(pooled.astype(h.dtype) @ w_groups[g])
    return jnp.concatenate(outs, axis=-1) * scale


def t5_causal_bucket(rel):
    n = jnp.maximum(rel, 0)
    max_exact = N_BUCKETS // 2
    nf = jnp.maximum(n, max_exact).astype(jnp.float32)
    large = max_exact + (jnp.log(nf / max_exact) / math.log(MAX_DISTANCE / max_exact)
                         * (N_BUCKETS - max_exact)).astype(jnp.int32)
    large = jnp.minimum(large, N_BUCKETS - 1)
    return jnp.where(n < max_exact, n, large)


def diff_attention(h, w_qkv, w_o, lq1, lk1, lq2, lk2, subln_g, rel_bias, lambda_init):
    B, L, D = h.shape
    Lp = ((L + Q_BLOCK - 1) // Q_BLOCK) * Q_BLOCK
    pad = ((0, 0), (0, Lp - L), (0, 0))
    qkv = h @ w_qkv
    q, k, v = jnp.split(qkv, 3, axis=-1)
    q = jnp.pad(q, pad).reshape(B, Lp, N_HEADS, 2, HEAD_DIM).transpose(0, 2, 3, 1, 4)
    k = jnp.pad(k, pad).reshape(B, Lp, N_HEADS, 2, HEAD_DIM).transpose(0, 2, 3, 1, 4)
    v = jnp.pad(v, pad).reshape(B, Lp, N_HEADS, V_DIM).transpose(0, 2, 1, 3).astype(jnp.float32)
    lam = (jnp.exp(jnp.sum(lq1.astype(jnp.float32) * lk1.astype(jnp.float32)))
           - jnp.exp(jnp.sum(lq2.astype(jnp.float32) * lk2.astype(jnp.float32))) + lambda_init)
    scale = 1.0 / math.sqrt(HEAD_DIM)
    kpos = jnp.arange(Lp)

    def block(i):
        q0 = i * Q_BLOCK
        qb = lax.dynamic_slice_in_dim(q, q0, Q_BLOCK, axis=3)
        rel = (q0 + jnp.arange(Q_BLOCK))[:, None] - kpos[None, :]
        bias = rel_bias[t5_causal_bucket(rel)].astype(jnp.float32).transpose(2, 0, 1)
        s = jnp.einsum('bhmqd,bhmkd->bhmqk', qb, k).astype(jnp.float32) * scale + bias[None, :, None]
        s = jnp.where(rel >= 0, s, NEG_INF)
        p = jax.nn.softmax(s, axis=-1)
        a = p[:, :, 0] - lam * p[:, :, 1]
        return jnp.einsum('bhqk,bhkv->bhqv', a, v)

    o = lax.map(block, jnp.arange(Lp // Q_BLOCK))
    o = o.transpose(1, 0, 3, 2, 4).reshape(B, Lp, N_HEADS, V_DIM)[:, :L]
    o = rms_norm(o, subln_g) * (1.0 - lambda_init)
    return o.reshape(B, L, D).astype(h.dtype) @ w_o


def swiglu(h, w_gate, w_up, w_down):
    return (jax.nn.silu(h @ w_gate) * (h @ w_up)) @ w_down


def setup_inputs(seed: int = 0) -> dict:
    key = jax.random.key(seed)
    ks = jax.random.split(key, 20)
    f32 = jnp.float32
    nrm = lambda k, s, sc: jax.random.normal(k, s, f32) * sc
    D, F = D_MODEL, D_FF
    return {
        "x": nrm(ks[0], (BATCH, SEQ, D), 1.0),
        "meta_tokens": nrm(ks[1], (N_META, D), 1.0),
        "rel_bias": nrm(ks[2], (N_BUCKETS, N_HEADS), 0.5),
        "mix_norm_g": 1.0 + nrm(ks[3], (DEPTH, D), 0.05),
        "ffn_norm_g": 1.0 + nrm(ks[4], (DEPTH, D), 0.05),
        "pool_w": nrm(ks[5], (N_POOL_LAYERS, N_POOL_GROUPS, POOL_GC, POOL_GC), POOL_GC ** -0.5),
        "pool_scale": 1.0 + nrm(ks[6], (N_POOL_LAYERS, D), 0.1),
        "attn_w_qkv": nrm(ks[7], (N_ATTN_LAYERS, D, 3 * D), D ** -0.5),
        "attn_w_o": nrm(ks[8], (N_ATTN_LAYERS, D, D), D ** -0.5),
        "lambda_q1": nrm(ks[9], (N_ATTN_LAYERS, HEAD_DIM), 0.1),
        "lambda_k1": nrm(ks[10], (N_ATTN_LAYERS, HEAD_DIM), 0.1),
        "lambda_q2": nrm(ks[11], (N_ATTN_LAYERS, HEAD_DIM), 0.1),
        "lambda_k2": nrm(ks[12], (N_ATTN_LAYERS, HEAD_DIM), 0.1),
        "subln_g": 1.0 + nrm(ks[13], (N_ATTN_LAYERS, V_DIM), 0.05),
        "ffn_w_gate": nrm(ks[14], (DEPTH, D, F), D ** -0.5),
        "ffn_w_up": nrm(ks[15], (DEPTH, D, F), D ** -0.5),
        "ffn_w_down": nrm(ks[16], (DEPTH, F, D), F ** -0.5),
        "final_norm_g": 1.0 + nrm(ks[17], (D,), 0.05),
    }


def reference(x, meta_tokens, rel_bias, mix_norm_g, ffn_norm_g, pool_w, pool_scale,
              attn_w_qkv, attn_w_o, lambda_q1, lambda_k1, lambda_q2, lambda_k2, subln_g,
              ffn_w_gate, ffn_w_up, ffn_w_down, final_norm_g):
    B = x.shape[0]
    meta = jnp.broadcast_to(meta_tokens.astype(x.dtype)[None], (B, N_META, D_MODEL))
    h = jnp.concatenate([meta, x], axis=1)
    for i in range(DEPTH):
        hn = rms_norm(h, mix_norm_g[i])
        j = i // N_MIXERS
        if i % N_MIXERS == 0:
            h = h + pool_mixer(hn, pool_w[j], pool_scale[j])
        else:
            lambda_init = 0.8 - 0.6 * math.exp(-0.3 * i)
            h = h + diff_attention(hn, attn_w_qkv[j], attn_w_o[j], lambda_q1[j], lambda_k1[j],
                                   lambda_q2[j], lambda_k2[j], subln_g[j], rel_bias, lambda_init)
        h = h + swiglu(rms_norm(h, ffn_norm_g[i]), ffn_w_gate[i], ffn_w_up[i], ffn_w_down[i])
    h = rms_norm(h, final_norm_g)
    return h[:, N_META:]
```

```python
import math
import contextlib
import numpy as np
import ml_dtypes

import concourse.bass as bass
import concourse.mybir as mybir
from concourse.bass_utils import run_bass_kernel_spmd

F32 = mybir.dt.float32
BF16 = mybir.dt.bfloat16
AF = mybir.ActivationFunctionType
ALU = mybir.AluOpType

D = 1024
KD = 8
FF = 2816
KF = 22
NH = 8
NMETA = 16
SEQ = 4096
TCH = 512
LAMBDA_INIT = 0.8 - 0.6 * math.exp(-0.3 * 1)
EPS = 1e-6

S_POOL = 0
S_G0 = 2
S_D0 = 46
S_K = 68
S_V = 76
S_Q = 84
S_O = 92
S_G1 = 100
S_D1 = 144
NSLOT = 166

NRING = 7
NSTAGE = 2
NDMASEM = 20
EPOCH = 2000
SERIALIZE = False
DEFER_SSQ = True


def _esz(dt):
    return mybir.dt.size(dt)


class _Op:
    __slots__ = ("eng", "fn", "deps", "sig", "tok", "dma")


class Prog:
    def __init__(self, nc, es):
        self.nc = nc
        self.es = es
        self.ops = []
        self.track = {}
        self.untracked = set()

    def regions(self, ap):
        name = ap.tensor.name
        if name in self.untracked:
            return None
        esz = _esz(ap.dtype)
        dims = [list(d) for d in ap.ap]
        off = int(ap.offset)
        if str(ap.space) == "DRAM":
            ext = 1 + sum((c - 1) * abs(s) for s, c in dims)
            return name, [(0, 1, off * esz, (off + ext) * esz)]
        pstep, pcnt = dims[0]
        if pstep == 0:
            pstep = 1 << 40
        p0 = off // pstep
        f0 = off % pstep
        free = [d for d in dims[1:] if d[1] > 1]
        if not free:
            return name, [(p0, p0 + pcnt, f0 * esz, (f0 + 1) * esz)]
        free.sort(key=lambda d: -abs(d[0]))
        segs = [f0]
        k = 0
        while k < len(free) - 1 and len(segs) * free[k][1] <= 32:
            s, c = free[k]
            segs = [b + i * s for b in segs for i in range(c)]
            k += 1
        ext = 1 + sum((c - 1) * abs(s) for s, c in free[k:])
        return name, [(p0, p0 + pcnt, b * esz, (b + ext) * esz) for b in segs]

    def add(self, eng, fn, reads=(), writes=(), dma=False):
        op = _Op()
        op.eng, op.fn, op.dma, op.sig, op.tok = eng, fn, dma, False, None
        oid = len(self.ops)
        deps = set()
        for kind, aps in (("R", reads), ("W", writes)):
            for ap in aps:
                r = self.regions(ap)
                if r is None:
                    continue
                name, ivs = r
                lst = self.track.setdefault(name, [])
                for (p0, p1, b0, b1) in ivs:
                    for e in lst:
                        if e[0] < p1 and p0 < e[1] and e[2] < b1 and b0 < e[3]:
                            if kind == "W" or e[4] == "W":
                                deps.add(e[5])
                if kind == "W":
                    keep = []
                    for e in lst:
                        cov = False
                        for (p0, p1, b0, b1) in ivs:
                            if p0 <= e[0] and e[1] <= p1 and b0 <= e[2] and e[3] <= b1:
                                cov = True
                                break
                        if not cov:
                            keep.append(e)
                    lst[:] = keep
                    for (p0, p1, b0, b1) in ivs:
                        lst.append((p0, p1, b0, b1, "W", oid, eng, dma))
                else:
                    for (p0, p1, b0, b1) in ivs:
                        if not dma:
                            lst[:] = [e for e in lst if not (
                                e[4] == "R" and e[6] == eng and not e[7] and
                                p0 <= e[0] and e[1] <= p1 and b0 <= e[2] and e[3] <= b1)]
                        lst.append((p0, p1, b0, b1, "R", oid, eng, dma))
        if SERIALIZE and oid > 0:
            deps.add(oid - 1)
        deps.discard(oid)
        op.deps = sorted(deps)
        for d in op.deps:
            p = self.ops[d]
            if p.eng == "pe" and eng == "pe" and not p.dma and not dma:
                continue
            p.sig = True
        self.ops.append(op)
        return oid

    def emit(self, final_wait_ops):
        nc = self.nc
        engs = {"pe": nc.tensor, "act": nc.scalar, "dve": nc.vector, "pool": nc.gpsimd, "sp": nc.sync}
        cnt = {e: 0 for e in engs}
        sems = {e: [] for e in engs}
        waited = {e: {} for e in engs}
        dma_sems = [self.es.enter_context(nc.semaphore("dq%d" % i)) for i in range(NDMASEM)]
        dma_n = 0
        nwait = 0

        def wait(eng, tok):
            nonlocal nwait
            sem, val = tok
            w = waited[eng]
            if w.get(sem.num, 0) >= val:
                return
            engs[eng].wait_ge(sem, val)
            w[sem.num] = val
            nwait += 1

        for op in self.ops:
            E = engs[op.eng]
            for d in op.deps:
                p = self.ops[d]
                if p.eng == "pe" and op.eng == "pe" and not p.dma and not op.dma:
                    continue
                wait(op.eng, p.tok)
            if op.dma:
                k = dma_n % NDMASEM
                prev = dma_n // NDMASEM
                sem = dma_sems[k]
                if prev > 0:
                    wait(op.eng, (sem, 16 * prev))
                inst = op.fn(E)
                inst.then_inc(sem, 16)
                op.tok = (sem, 16 * (prev + 1))
                dma_n += 1
            else:
                inst = op.fn(E)
                if op.sig:
                    c = cnt[op.eng]
                    ep = c // EPOCH
                    if ep >= len(sems[op.eng]):
                        sems[op.eng].append(self.es.enter_context(
                            nc.semaphore("e_%s_%d" % (op.eng, ep))))
                    sem = sems[op.eng][ep]
                    inst.then_inc(sem, 1)
                    op.tok = (sem, c % EPOCH + 1)
                    cnt[op.eng] = c + 1
        for oid in final_wait_ops:
            wait("sp", self.ops[oid].tok)
        return dict(nops=len(self.ops), nwait=nwait, ndma=dma_n, sig=dict(cnt))


def build(nch, dbg=None):
    nc = bass.Bass("TRN2", target_bir_lowering=False, dynamic_dma_scratch_size=256)
    es = contextlib.ExitStack()
    P = Prog(nc, es)
    ntok = nch * TCH
    npos = NMETA + ntok
    nblk = nch * 4

    def din(name, shape, dt=F32):
        return nc.dram_tensor(name, list(shape), dt, kind="ExternalInput").ap()

    x_d = din("x", [ntok, D])
    meta_d = din("meta", [NMETA, D])
    wraw_d = din("wraw", [NSLOT, 128, 1024])
    gv_d = din("gv", [128, 4 * KD])
    subln_d = din("subln", [128, 1])
    pscale_d = din("pscale", [128, D])
    gfin_d = din("gfin", [128, D])
    lam_d = din("lamv", [128, 4 * 64])
    braw_d = din("braw", [128, NH * 3 * 128])
    cfar_d = din("cfar", [128, NH])
    cbf_d = din("cbf", [128, 128 * 2 + 128 + 1664], BF16)
    y_d = nc.dram_tensor("y", [ntok, D], F32, kind="ExternalOutput").ap()
    wall_d = nc.dram_tensor("wall", [NSLOT, 128, 1024], BF16, kind="Internal").ap()

    def sb(name, shape, dt):
        return es.enter_context(nc.sbuf_tensor(name, list(shape), dt))

    KT = sb("KT", [128, NH, npos], BF16)
    VC = sb("VC", [128, nblk, D], BF16)
    VM = sb("VM", [NMETA, D], BF16)
    H = sb("H", [128, 4, D], F32)
    HS = sb("HS", [128, 3, D], BF16)
    HST = sb("HST", [128, KD, TCH], BF16)
    U = sb("U", [128, KF * TCH], BF16)
    RING = sb("RING", [128, NRING, 1024], BF16)
    STG = sb("STG", [128, NSTAGE, 1024], F32)
    SIL = sb("SIL", [128, 2, TCH], F32)
    MT = sb("MT", [128, NH, 3, 128], BF16)
    CB = sb("CB", [128, 128 * 3 + 1664], BF16)
    GF = sb("GF", [128, D], F32)
    PSC = None
    GV = sb("GV", [128, 4, KD], F32)
    SM = sb("SM", [128, 64], F32)
    PS = es.enter_context(nc.psum_tensor("PS", [128, 8, 512], F32))

    ident = CB[:, 0:128]
    ones = CB[:, 128:256]
    dmask = CB[:, 256:384]
    AOFF = 384

    def a_main(g):
        return CB[:, AOFF + g * 128: AOFF + (g + 1) * 128]

    def a_halo(g):
        return CB[:, AOFF + 512 + g * 128: AOFF + 512 + (g + 1) * 128]

    def a_halometa(g):
        return CB[0:16, AOFF + 1024 + g * 128: AOFF + 1024 + (g + 1) * 128]

    def a_first(g, lo):
        o = AOFF + 1536 + (g * 2 + lo) * 16
        return CB[0:16, o:o + 16]

    ACT_T = U[:, :].rearrange("p (f t) -> p f t", t=TCH)
    QT = U[:, 0:1024].rearrange("p (a t) -> p a t", t=TCH)
    OT = U[:, 1024:5120].rearrange("p (a t) -> p a t", t=TCH)
    PT = U[:, 5120:7168].rearrange("p (a t) -> p a t", t=TCH)
    SQ = U[:, 7168:7680]
    TT = U[:, 7680:9728].bitcast(F32).rearrange("p (a t) -> p a t", t=TCH)
    OUTT = U[:, 0:4096].bitcast(F32).rearrange("p (a t) -> p a t", t=D)
    BRAW = U[:, 0:6144].bitcast(F32).rearrange("p (h k t) -> p h k t", h=NH, k=3)
    PSCL = U[:, 6144:8192].bitcast(F32)
    LAMV = U[:, 8192:8704].bitcast(F32)

    SSQ = lambda j: SM[:, j:j + 1]
    LNV = lambda j: SM[:, 4 + j:5 + j]
    RSTD = lambda j: SM[:, 8 + j:9 + j]
    NEGLAM = SM[:, 12:13]
    LS = lambda j: SM[:, 13 + j:14 + j]
    CF = SM[:, 20:28]
    SUBL = SM[:, 28:29]
    EPSC = SM[:, 29:30]

    def dma(out, in_):
        return P.add("sp", lambda e: e.dma_start(out=out, in_=in_), reads=[in_], writes=[out], dma=True)

    def mm(out, lhsT, rhs, start, stop):
        P.add("pe", lambda e: e.matmul(out, lhsT, rhs, start=start, stop=stop, skip_group_check=True),
              reads=[lhsT, rhs], writes=[out])

    def act(out, in_, func, bias=None, scale=None, extra_reads=()):
        kw = {}
        if bias is not None:
            kw["bias"] = bias
        if scale is not None:
            kw["scale"] = scale
        rd = [in_] + list(extra_reads)
        if bias is not None and not isinstance(bias, (int, float)):
            rd.append(bias)
        P.add("act", lambda e: e.activation(out=out, in_=in_, func=func, **kw), reads=rd, writes=[out])

    def tt(out, in0, in1, op, eng="dve"):
        P.add(eng, lambda e: e.tensor_tensor(out=out, in0=in0, in1=in1, op=op),
              reads=[in0, in1], writes=[out])

    def ts(out, in0, s1, op0, s2=None, op1=None, eng="dve"):
        rd = [in0]
        if not isinstance(s1, (int, float)):
            rd.append(s1)
        if s2 is not None and not isinstance(s2, (int, float)):
            rd.append(s2)
        if op1 is None:
            P.add(eng, lambda e: e.tensor_scalar(out=out, in0=in0, scalar1=s1, scalar2=None, op0=op0),
                  reads=rd, writes=[out])
        else:
            P.add(eng, lambda e: e.tensor_scalar(out=out, in0=in0, scalar1=s1, scalar2=s2, op0=op0, op1=op1),
                  reads=rd, writes=[out])

    def stt(out, in0, scalar, in1, op0, op1):
        rd = [in0, in1]
        if not isinstance(scalar, (int, float)):
            rd.append(scalar)
        P.add("dve", lambda e: e.scalar_tensor_tensor(out=out, in0=in0, scalar=scalar, in1=in1, op0=op0, op1=op1),
              reads=rd, writes=[out])

    def copy(out, in_, eng):
        if eng == "act":
            P.add("act", lambda e: e.copy(out=out, in_=in_), reads=[in_], writes=[out])
        else:
            P.add(eng, lambda e: e.tensor_copy(out=out, in_=in_), reads=[in_], writes=[out])

    bank_ctr = [0]

    def next_bank():
        b = bank_ctr[0] % 8
        bank_ctr[0] += 1
        return b

    ring_ctr = [0]
    stage_of = {}
    stage_ctr = [0]
    converted = [False] * NSLOT

    def ensure_loaded(s):
        if s >= NSLOT or converted[s] or s in stage_of:
            return
        st = stage_ctr[0] % NSTAGE
        stage_ctr[0] += 1
        stage_of[s] = st
        dma(STG[:, st, :], wraw_d[s])

    def bc_last(ap2, n):
        d = [list(x) for x in ap2.ap]
        return bass.AP(ap2.tensor, ap2.offset, d + [[0, n]])

    def convert(s, st, dst):
        src = STG[:, st, :]
        if s < S_G0:
            for gl in range(2):
                g = 2 * s + gl
                for kc in range(2):
                    o = (gl * 2 + kc) * 256
                    stt(dst[:, o:o + 256], src[:, o:o + 256], GV[:, 0, 2 * g + kc:2 * g + kc + 1],
                        PSCL[:, g * 256:(g + 1) * 256], ALU.mult, ALU.mult)
            return
        if S_D0 <= s < S_K or s >= S_D1:
            copy(dst, src, "dve")
            return
        if S_V <= s < S_Q:
            k = s - S_V
            ts(dst, src, GV[:, 2, k:k + 1], ALU.mult)
            return
        if S_O <= s < S_G1:
            ts(dst, src, SUBL, ALU.mult, 1.0 - LAMBDA_INIT, ALU.mult)
            return
        if s < S_D0:
            a = 1
        elif s < S_O:
            a = 2
        else:
            a = 3
        d3 = dst.rearrange("p (k j) -> p k j", j=128)
        s3 = src.rearrange("p (k j) -> p k j", j=128)
        tt(d3, s3, bc_last(GV[:, a, :], 128), ALU.mult)

    def get_slot(s):
        r = ring_ctr[0] % NRING
        ring_ctr[0] += 1
        dst = RING[:, r, :]
        if not converted[s]:
            ensure_loaded(s)
            ensure_loaded(s + 1)
            convert(s, stage_of.pop(s), dst)
            dma(wall_d[s], dst)
            converted[s] = True
        else:
            dma(dst, wall_d[s])
        return dst

    dma(CB[:, :], cbf_d)
    dma(GV[:, :, :].rearrange("p a k -> p (a k)"), gv_d)
    dma(SUBL, subln_d)
    dma(GF[:, :], gfin_d)
    dma(CF, cfar_d)
    dma(PSCL, pscale_d)
    dma(LAMV, lam_d)
    dma(BRAW.rearrange("p h k t -> p (h k t)"), braw_d)
    P.add("dve", lambda e: e.memset(EPSC, EPS), writes=[EPSC])
    for j in range(2):
        a0 = LAMV[:, (2 * j) * 64:(2 * j + 1) * 64]
        a1 = LAMV[:, (2 * j + 1) * 64:(2 * j + 2) * 64]
        P.add("dve", lambda e, a0=a0, a1=a1, j=j: e.scalar_tensor_tensor(
            out=HS[:, 0, 0:64], in0=a0, scalar=1.0, in1=a1,
            op0=ALU.mult, op1=ALU.mult, accum_out=LS(j)), reads=[a0, a1], writes=[LS(j), HS[:, 0, 0:64]])
        act(LS(2 + j), LS(j), AF.Exp)
    stt(NEGLAM, LS(3), -LAMBDA_INIT, LS(2), ALU.add, ALU.subtract)
    for h in range(NH):
        for k in range(3):
            ts(BRAW[:, h, k, :], BRAW[:, h, k, :], CF[:, h:h + 1], ALU.subtract)
            act(MT[:, h, k, :], BRAW[:, h, k, :], AF.Exp)
        tt(MT[:, h, 0, :], MT[:, h, 0, :], dmask, ALU.mult)

    def norm_tile(i, R, dst):
        hin = H[0:R, i, :]
        P.add("act", lambda e: e.activation(
            out=dst, in_=hin, func=AF.Square, accum_out=SM[0:R, i:i + 1]),
            reads=[hin], writes=[SM[0:R, i:i + 1], dst])
        act(SM[0:R, 4 + i:5 + i], SM[0:R, i:i + 1], AF.Ln, bias=EPSC[0:R, :], scale=1.0 / D)
        act(SM[0:R, 8 + i:9 + i], SM[0:R, 4 + i:5 + i], AF.Exp, scale=-0.5)
        ts(dst, hin, SM[0:R, 8 + i:9 + i], ALU.mult)

    hs_ctr = [0]

    def norm_to_hsT(tiles):
        for i, R in tiles:
            slot = hs_ctr[0] % 2
            hs_ctr[0] += 1
            hs = HS[0:R, slot, :]
            norm_tile(i, R, hs)
            for half in range(2):
                b = next_bank()
                for q in range(4):
                    kc = half * 4 + q
                    mm(PS[:, b, q * 128:q * 128 + R], hs[:, kc * 128:(kc + 1) * 128], ident[0:R, 0:R], True, True)
                src = PS[:, b, :].rearrange("p (q t) -> p q t", t=128)[:, :, 0:R]
                copy(HST[:, half * 4:half * 4 + 4, i * 128:i * 128 + R], src, "act")

    def acc8(tiles, nk, lhs_fn, slot_fn, evac_fn):
        for k in range(nk):
            w = slot_fn(k)
            for ti, (i, R) in enumerate(tiles):
                for n in range(2):
                    mm(PS[0:R, ti * 2 + n, :], lhs_fn(k, i, R), w[:, n * 512:(n + 1) * 512], k == 0, k == nk - 1)
        for ti, (i, R) in enumerate(tiles):
            for n in range(2):
                evac_fn(i, R, n, PS[0:R, ti * 2 + n, :])

    def resid_add(i, R, n, ps):
        hv = H[0:R, i, n * 512:(n + 1) * 512]
        tt(hv, ps, hv, ALU.add)

    def ffn(layer, tiles, T):
        norm_to_hsT(tiles)
        sg = S_G0 if layer == 0 else S_G1
        sd = S_D0 if layer == 0 else S_D1
        for f in range(KF):
            wg = get_slot(sg + 2 * f)
            wu = get_slot(sg + 2 * f + 1)
            bg = next_bank()
            bu = next_bank()
            for k in range(KD):
                mm(PS[:, bg, 0:T], wg[:, k * 128:(k + 1) * 128], HST[:, k, 0:T], k == 0, k == KD - 1)
            for k in range(KD):
                mm(PS[:, bu, 0:T], wu[:, k * 128:(k + 1) * 128], HST[:, k, 0:T], k == 0, k == KD - 1)
            st = SIL[:, f % 2, 0:T]
            act(st, PS[:, bg, 0:T], AF.Silu)
            tt(ACT_T[:, f, 0:T], st, PS[:, bu, 0:T], ALU.mult)
        acc8(tiles, KF,
             lambda k, i, R: ACT_T[:, k, i * 128:i * 128 + R],
             lambda k: get_slot(sd + k), resid_add)

    def pool_layer(tiles, is_meta, first_chunk):
        wp = [get_slot(S_POOL), get_slot(S_POOL + 1)]
        for (i, R) in tiles:
            slot = hs_ctr[0] % 2
            hs_ctr[0] += 1
            hs = HS[0:R, slot, :]
            norm_tile(i, R, hs)
            if is_meta:
                halo = None
            elif i == 0:
                halo = HS[0:16, 2, :] if first_chunk else HS[:, 2, :]
            else:
                halo = HS[:, (slot + 1) % 2, :]
            for half in range(2):
                b = next_bank()
                for q in range(4):
                    kc = half * 4 + q
                    g = kc // 2
                    o = PS[:, b, q * 128:q * 128 + R]
                    lhs = hs[:, kc * 128:(kc + 1) * 128]
                    if is_meta:
                        mm(o, lhs, a_first(g, 0), True, False)
                        mm(o, lhs, a_first(g, 1), False, True)
                    else:
                        mm(o, lhs, a_main(g), True, False)
                        if i == 0 and first_chunk:
                            mm(o, halo[:, kc * 128:(kc + 1) * 128], a_halometa(g), False, True)
                        else:
                            mm(o, halo[:, kc * 128:(kc + 1) * 128], a_halo(g), False, True)
                src = PS[:, b, :].rearrange("p (q t) -> p q t", t=128)[:, :, 0:R]
                copy(HST[:, half * 4:half * 4 + 4, i * 128:i * 128 + R], src, "act")
            if is_meta:
                copy(HS[0:16, 2, :], hs, "pool")
            elif i == 3:
                copy(HS[:, 2, :], hs, "pool")
            b0 = next_bank()
            b1 = next_bank()
            for g in range(4):
                bb = b0 if g < 2 else b1
                for kk in range(2):
                    w = wp[g // 2][:, ((g % 2) * 2 + kk) * 256:((g % 2) * 2 + kk + 1) * 256]
                    mm(PS[0:R, bb, (g % 2) * 256:(g % 2 + 1) * 256],
                       HST[:, 2 * g + kk, i * 128:i * 128 + R], w, kk == 0, kk == 1)
            resid_add(i, R, 0, PS[0:R, b0, :])
            resid_add(i, R, 1, PS[0:R, b1, :])

    def kv_proj(tiles, T, pos0, blk0):
        norm_to_hsT(tiles)
        for h in range(NH):
            w = get_slot(S_K + h)
            b = next_bank()
            for k in range(KD):
                mm(PS[:, b, 0:T], w[:, k * 128:(k + 1) * 128], HST[:, k, 0:T], k == 0, k == KD - 1)
            copy(KT[:, h, pos0:pos0 + T], PS[:, b, 0:T], "act")

        def v_evac(i, R, n, ps):
            if blk0 is None:
                copy(VM[0:R, n * 512:(n + 1) * 512], ps, "dve")
            else:
                copy(VC[:, blk0 + i, n * 512:(n + 1) * 512], ps, "dve")
        acc8(tiles, KD, lambda k, i, R: HST[:, k, i * 128:i * 128 + R],
             lambda k: get_slot(S_V + k), v_evac)

    s_ctr = [0]

    def attention(c):
        pos0 = NMETA + c * TCH
        pend = [None]

        def emit_ssq(h):
            b = s_ctr[0] % 4
            s_ctr[0] += 1
            mm(PS[:, b, :], ones, SQ, True, True)
            act(TT[:, 1, :], PS[:, b, :], AF.Ln, bias=EPSC, scale=1.0 / 128)
            act(TT[:, 1, :], TT[:, 1, :], AF.Exp, scale=-0.5)
            tt(OT[:, h, :], TT[:, 0, :], TT[:, 1, :], ALU.mult)

        for h in range(NH):
            w = get_slot(S_Q + h)
            b = s_ctr[0] % 4
            s_ctr[0] += 1
            for k in range(KD):
                mm(PS[:, b, :], w[:, k * 128:(k + 1) * 128], HST[:, k, :], k == 0, k == KD - 1)
            qt = QT[:, h % 2, :]
            copy(qt, PS[:, b, :], "dve")
            blocks = [("m", 0)] + [("f", j) for j in range(4 * c)] + [("d", jj) for jj in (3, 2, 1, 0)]
            items = [(blk, m) for blk in blocks for m in (0, 1)]
            n = len(items)
            sb_of = {}

            def qk(ix):
                (kind, j), m = items[ix]
                b = s_ctr[0] % 4
                s_ctr[0] += 1
                sb_of[ix] = b
                pr = slice(64 * m, 64 * m + 64)
                if kind == "m":
                    mm(PS[0:16, b, :], KT[pr, h, 0:16], qt[pr, :], True, True)
                elif kind == "f":
                    k0 = NMETA + j * 128
                    mm(PS[:, b, :], KT[pr, h, k0:k0 + 128], qt[pr, :], True, True)
                else:
                    k0 = pos0 + j * 128
                    mm(PS[:, b, j * 128:512], KT[pr, h, k0:k0 + 128], qt[pr, j * 128:512], True, True)

            LA = 2
            for ix in range(min(LA, n)):
                qk(ix)
            for ix in range(n):
                if ix + LA < n:
                    qk(ix + LA)
                (kind, j), m = items[ix]
                b = sb_of[ix]
                pb = ix % 4
                first = ix < 2
                last = ix >= n - 2
                if kind == "m":
                    KR, q0 = 16, 0
                elif kind == "f":
                    KR, q0 = 128, 0
                else:
                    KR, q0 = 128, j * 128
                pt = PT[0:KR, pb, q0:512]
                act(pt, PS[0:KR, b, q0:512], AF.Exp, scale=0.125)
                if kind == "m" and c == 0:
                    tt(PT[0:16, pb, 0:128], PT[0:16, pb, 0:128], MT[0:16, h, 2, :], ALU.mult)
                if kind == "f" and j == 4 * c - 1:
                    tt(PT[:, pb, 0:128], PT[:, pb, 0:128], MT[:, h, 1, :], ALU.mult)
                if kind == "d":
                    tt(PT[:, pb, q0:q0 + 128], PT[:, pb, q0:q0 + 128], MT[:, h, 0, :], ALU.mult)
                    if j < 3:
                        tt(PT[:, pb, q0 + 128:q0 + 256], PT[:, pb, q0 + 128:q0 + 256], MT[:, h, 1, :], ALU.mult)
                if kind == "m":
                    vv = VM[0:16, h * 128:(h + 1) * 128]
                elif kind == "f":
                    vv = VC[:, j, h * 128:(h + 1) * 128]
                else:
                    vv = VC[:, 4 * c + j, h * 128:(h + 1) * 128]
                mm(PS[:, 4 + m, q0:512], vv, pt, first, last)
                mm(PS[:, 6 + m, q0:512], ones[0:KR, :], pt, first, last)
                if ix == 1 and pend[0] is not None:
                    emit_ssq(pend[0])
                    pend[0] = None
            if pend[0] is not None:
                emit_ssq(pend[0])
                pend[0] = None
            for m in range(2):
                act(TT[:, m, :], PS[:, 6 + m, :], AF.Ln)
                act(TT[:, m, :], TT[:, m, :], AF.Exp, scale=-1.0)
                tt(TT[:, m, :], PS[:, 4 + m, :], TT[:, m, :], ALU.mult)
            stt(TT[:, 0, :], TT[:, 1, :], NEGLAM, TT[:, 0, :], ALU.mult, ALU.add)
            act(SQ, TT[:, 0, :], AF.Square)
            pend[0] = h
            if not DEFER_SSQ:
                emit_ssq(h)
                pend[0] = None
        if pend[0] is not None:
            emit_ssq(pend[0])

    def wo_proj(tiles):
        acc8(tiles, NH, lambda k, i, R: OT[:, k, i * 128:i * 128 + R],
             lambda k: get_slot(S_O + k), resid_add)

    out_ops = []

    def final_out(c, tiles):
        for (i, R) in tiles:
            hin = H[0:R, i, :]
            jk = HS[:, hs_ctr[0] % 2, :]
            hs_ctr[0] += 1
            P.add("act", lambda e, hin=hin, i=i, jk=jk: e.activation(
                out=jk, in_=hin, func=AF.Square, accum_out=SM[:, i:i + 1]),
                reads=[hin], writes=[SM[:, i:i + 1], jk])
            act(SM[:, 4 + i:5 + i], SM[:, i:i + 1], AF.Ln, bias=EPSC, scale=1.0 / D)
            act(SM[:, 8 + i:9 + i], SM[:, 4 + i:5 + i], AF.Exp, scale=-0.5)
            o = OUTT[:, i % 2, :]
            stt(o, hin, SM[:, 8 + i:9 + i], GF[:, :], ALU.mult, ALU.mult)
            r0 = c * TCH + i * 128
            out_ops.append(dma(y_d[r0:r0 + 128, :], o))

    mt = [(0, NMETA)]
    dma(H[0:NMETA, 0, :], meta_d)
    pool_layer(mt, True, False)
    ffn(0, mt, NMETA)
    kv_proj(mt, NMETA, 0, None)

    tiles = [(i, 128) for i in range(4)]
    for c in range(nch):
        for i in range(4):
            r0 = c * TCH + i * 128
            dma(H[:, i, :], x_d[r0:r0 + 128, :])
        pool_layer(tiles, False, c == 0)
        ffn(0, tiles, TCH)
        kv_proj(tiles, TCH, NMETA + c * TCH, 4 * c)
        def dump():
            for i in range(4):
                r0 = c * TCH + i * 128
                out_ops.append(dma(y_d[r0:r0 + 128, :], H[:, i, :]))
        if dbg == "ffn0":
            dump()
            continue
        if dbg == "vc":
            for i in range(4):
                copy(OUTT[:, i % 2, :], VC[:, 4 * c + i, :], "dve")
                out_ops.append(dma(y_d[c * TCH + i * 128:c * TCH + (i + 1) * 128, :], OUTT[:, i % 2, :]))
            continue
        if dbg == "kt":
            for hh in range(4):
                copy(OUTT[:, hh % 2, 0:512], KT[:, hh, NMETA + c * TCH:NMETA + (c + 1) * TCH], "dve")
                out_ops.append(dma(y_d[c * TCH + hh * 128:c * TCH + (hh + 1) * 128, 0:512], OUTT[:, hh % 2, 0:512]))
            continue
        attention(c)
        if dbg == "ot":
            for i in range(4):
                for hh in range(NH):
                    copy(STG[:, i % 2, hh * 128:(hh + 1) * 128], OT[:, hh, i * 128:(i + 1) * 128], "dve")
                out_ops.append(dma(y_d[c * TCH + i * 128:c * TCH + (i + 1) * 128, :], STG[:, i % 2, :]))
            continue
        wo_proj(tiles)
        if dbg == "wo":
            dump()
            continue
        ffn(1, tiles, TCH)
        final_out(c, tiles)

    stats = P.emit(out_ops)
    es.close()
    return nc, stats


def _t5_bucket(rel):
    n = np.maximum(rel, 0)
    max_exact = 16
    nf = np.maximum(n, max_exact).astype(np.float32)
    large = max_exact + (np.log(nf / max_exact) / math.log(128 / max_exact) * (32 - max_exact)).astype(np.int32)
    large = np.minimum(large, 31)
    return np.where(n < max_exact, n, large)


def _const_bf16():
    c = np.zeros((128, 128 * 3 + 1664), np.float32)
    c[:, 0:128] = np.eye(128)
    c[:, 128:256] = 1.0
    kl = np.arange(128)[:, None]
    ql = np.arange(128)[None, :]
    c[:, 256:384] = (ql >= kl)
    A = 384
    for g, w in enumerate((2, 4, 8, 16)):
        tp = np.arange(128)[:, None]
        t = np.arange(128)[None, :]
        main = ((tp <= t) & (tp >= t - w + 1)) * (1.0 / w) - (tp == t) * 1.0
        c[:, A + g * 128:A + (g + 1) * 128] = main
        halo = ((tp - 128) >= (t - w + 1)) * (1.0 / w)
        c[:, A + 512 + g * 128:A + 512 + (g + 1) * 128] = halo
        tpm = np.arange(16)[:, None]
        hm = ((tpm - 16) >= (t - w + 1)) * (1.0 / w)
        c[0:16, A + 1024 + g * 128:A + 1024 + (g + 1) * 128] = hm
        t16 = np.arange(16)[None, :]
        cnt = np.minimum(t16 + 1, w).astype(np.float64)
        first = ((tpm <= t16) & (tpm >= t16 - w + 1)) / cnt - (tpm == t16) * 1.0
        hi = first.astype(np.float32).astype(ml_dtypes.bfloat16).astype(np.float32)
        lo = (first - hi).astype(np.float32)
        o = A + 1536 + (g * 2) * 16
        c[0:16, o:o + 16] = hi
        c[0:16, o + 16:o + 32] = lo
    return c.astype(ml_dtypes.bfloat16)


def _weight_slots(inp):
    S = np.empty((NSLOT, 128, 1024), np.float32)
    pw = np.asarray(inp["pool_w"], np.float32)[0]
    for sl in range(2):
        blk = pw[2 * sl:2 * sl + 2].reshape(2, 2, 128, 256).transpose(2, 0, 1, 3)
        S[S_POOL + sl] = blk.reshape(128, 1024)

    def colk(w, ncol):
        return w.reshape(KD, 128, ncol, 128).transpose(2, 1, 0, 3).reshape(ncol, 128, 1024)

    for l, (sg, sd) in enumerate(((S_G0, S_D0), (S_G1, S_D1))):
        wg = colk(np.asarray(inp["ffn_w_gate"], np.float32)[l], KF)
        wu = colk(np.asarray(inp["ffn_w_up"], np.float32)[l], KF)
        S[sg:sg + 2 * KF:2] = wg
        S[sg + 1:sg + 2 * KF:2] = wu
        S[sd:sd + KF] = np.asarray(inp["ffn_w_down"], np.float32)[l].reshape(KF, 128, 1024)
    wqkv = np.asarray(inp["attn_w_qkv"], np.float32)[0]
    S[S_Q:S_Q + NH] = colk(wqkv[:, 0:1024], NH)
    S[S_K:S_K + NH] = colk(wqkv[:, 1024:2048], NH)
    S[S_V:S_V + KD] = wqkv[:, 2048:3072].reshape(KD, 128, 1024)
    S[S_O:S_O + NH] = np.asarray(inp["attn_w_o"], np.float32)[0].reshape(NH, 128, 1024)
    return S


def _host_inputs(inp, nch):
    f32 = lambda a: np.ascontiguousarray(np.asarray(a, np.float32))
    rb = f32(inp["rel_bias"])
    kl = np.arange(128)[:, None]
    ql = np.arange(128)[None, :]
    braw = np.zeros((128, NH, 3, 128), np.float32)
    bd = _t5_bucket(ql - kl)
    bs = _t5_bucket(128 + ql - kl)
    bm = _t5_bucket(16 + ql - kl)
    for h in range(NH):
        braw[:, h, 0, :] = rb[bd, h]
        braw[:, h, 1, :] = rb[bs, h]
        braw[:, h, 2, :] = rb[bm, h]
    gv = np.stack([f32(inp["mix_norm_g"])[0], f32(inp["ffn_norm_g"])[0],
                   f32(inp["mix_norm_g"])[1], f32(inp["ffn_norm_g"])[1]], 0)
    gv = gv.reshape(4, KD, 128).transpose(2, 0, 1).reshape(128, 4 * KD)
    lamv = np.concatenate([f32(inp["lambda_q1"])[0], f32(inp["lambda_k1"])[0],
                           f32(inp["lambda_q2"])[0], f32(inp["lambda_k2"])[0]])
    shared = {
        "meta": f32(inp["meta_tokens"]),
        "wraw": _weight_slots(inp),
        "gv": np.ascontiguousarray(gv),
        "subln": f32(inp["subln_g"])[0].reshape(128, 1).copy(),
        "pscale": np.ascontiguousarray(np.broadcast_to(f32(inp["pool_scale"])[0][None, :], (128, D))),
        "gfin": np.ascontiguousarray(np.broadcast_to(f32(inp["final_norm_g"])[None, :], (128, D))),
        "lamv": np.ascontiguousarray(np.broadcast_to(lamv[None, :], (128, 256))),
        "braw": braw.reshape(128, NH * 3 * 128),
        "cfar": np.ascontiguousarray(np.broadcast_to(rb[31][None, :], (128, NH))),
        "cbf": _const_bf16(),
    }
    return shared


_CACHE = {}


def run(inp, nch, ncores):
    if nch not in _CACHE:
        _CACHE[nch] = build(nch)
    nc, stats = _CACHE[nch]
    shared = _host_inputs(inp, nch)
    x = np.asarray(inp["x"], np.float32)
    in_maps = []
    for b in range(ncores):
        m = dict(shared)
        m["x"] = np.ascontiguousarray(x[b, :nch * TCH, :])
        in_maps.append(m)
    res = run_bass_kernel_spmd(nc, in_maps, core_ids=list(range(ncores)))
    return np.stack([np.asarray(r["y"]) for r in res.results], 0)


def kernel(**inputs):
    out = run(inputs, SEQ // TCH, 8)
    return out.astype(np.float32)
```

```python
import math
import contextlib
import numpy as np
import ml_dtypes

import concourse.bass as bass
import concourse.mybir as mybir
from concourse.bass_utils import run_bass_kernel_spmd

F32 = mybir.dt.float32
BF16 = mybir.dt.bfloat16
AF = mybir.ActivationFunctionType
ALU = mybir.AluOpType

D = 1024
KD = 8
FF = 2816
KF = 22
NH = 8
NMETA = 16
SEQ = 4096
TCH = 512
LAMBDA_INIT = 0.8 - 0.6 * math.exp(-0.3 * 1)
EPS = 1e-6

S_POOL = 0
S_G0 = 2
S_D0 = 46
S_K = 68
S_V = 76
S_Q = 84
S_O = 92
S_G1 = 100
S_D1 = 144
NSLOT = 166

NRING = 7
NSTAGE = 2
NDMASEM = 20
EPOCH = 2000
SERIALIZE = False
DEFER_SSQ = True
FIX_ENG = "pool"


def _esz(dt):
    return mybir.dt.size(dt)


class _Op:
    __slots__ = ("eng", "fn", "deps", "sig", "tok", "dma")


class Prog:
    def __init__(self, nc, es):
        self.nc = nc
        self.es = es
        self.ops = []
        self.track = {}
        self.untracked = set()

    def regions(self, ap):
        name = ap.tensor.name
        if name in self.untracked:
            return None
        esz = _esz(ap.dtype)
        dims = [list(d) for d in ap.ap]
        off = int(ap.offset)
        if str(ap.space) == "DRAM":
            ext = 1 + sum((c - 1) * abs(s) for s, c in dims)
            return name, [(0, 1, off * esz, (off + ext) * esz)]
        pstep, pcnt = dims[0]
        if pstep == 0:
            pstep = 1 << 40
        p0 = off // pstep
        f0 = off % pstep
        free = [d for d in dims[1:] if d[1] > 1]
        if not free:
            return name, [(p0, p0 + pcnt, f0 * esz, (f0 + 1) * esz)]
        free.sort(key=lambda d: -abs(d[0]))
        segs = [f0]
        k = 0
        while k < len(free) - 1 and len(segs) * free[k][1] <= 32:
            s, c = free[k]
            segs = [b + i * s for b in segs for i in range(c)]
            k += 1
        ext = 1 + sum((c - 1) * abs(s) for s, c in free[k:])
        return name, [(p0, p0 + pcnt, b * esz, (b + ext) * esz) for b in segs]

    def add(self, eng, fn, reads=(), writes=(), dma=False):
        op = _Op()
        op.eng, op.fn, op.dma, op.sig, op.tok = eng, fn, dma, False, None
        oid = len(self.ops)
        deps = set()
        for kind, aps in (("R", reads), ("W", writes)):
            for ap in aps:
                r = self.regions(ap)
                if r is None:
                    continue
                name, ivs = r
                lst = self.track.setdefault(name, [])
                for (p0, p1, b0, b1) in ivs:
                    for e in lst:
                        if e[0] < p1 and p0 < e[1] and e[2] < b1 and b0 < e[3]:
                            if kind == "W" or e[4] == "W":
                                deps.add(e[5])
                if kind == "W":
                    keep = []
                    for e in lst:
                        cov = False
                        for (p0, p1, b0, b1) in ivs:
                            if p0 <= e[0] and e[1] <= p1 and b0 <= e[2] and e[3] <= b1:
                                cov = True
                                break
                        if not cov:
                            keep.append(e)
                    lst[:] = keep
                    for (p0, p1, b0, b1) in ivs:
                        lst.append((p0, p1, b0, b1, "W", oid, eng, dma))
                else:
                    for (p0, p1, b0, b1) in ivs:
                        if not dma:
                            lst[:] = [e for e in lst if not (
                                e[4] == "R" and e[6] == eng and not e[7] and
                                p0 <= e[0] and e[1] <= p1 and b0 <= e[2] and e[3] <= b1)]
                        lst.append((p0, p1, b0, b1, "R", oid, eng, dma))
        if SERIALIZE and oid > 0:
            deps.add(oid - 1)
        deps.discard(oid)
        op.deps = sorted(deps)
        for d in op.deps:
            p = self.ops[d]
            if p.eng == "pe" and eng == "pe" and not p.dma and not dma:
                continue
            p.sig = True
        self.ops.append(op)
        return oid

    def emit(self, final_wait_ops):
        nc = self.nc
        engs = {"pe": nc.tensor, "act": nc.scalar, "dve": nc.vector, "pool": nc.gpsimd, "sp": nc.sync}
        cnt = {e: 0 for e in engs}
        sems = {e: [] for e in engs}
        waited = {e: {} for e in engs}
        dma_sems = [self.es.enter_context(nc.semaphore("dq%d" % i)) for i in range(NDMASEM)]
        dma_n = 0
        nwait = 0

        def wait(eng, tok):
            nonlocal nwait
            sem, val = tok
            w = waited[eng]
            if w.get(sem.num, 0) >= val:
                return
            engs[eng].wait_ge(sem, val)
            w[sem.num] = val
            nwait += 1

        for op in self.ops:
            E = engs[op.eng]
            for d in op.deps:
                p = self.ops[d]
                if p.eng == "pe" and op.eng == "pe" and not p.dma and not op.dma:
                    continue
                wait(op.eng, p.tok)
            if op.dma:
                k = dma_n % NDMASEM
                prev = dma_n // NDMASEM
                sem = dma_sems[k]
                if prev > 0:
                    wait(op.eng, (sem, 16 * prev))
                inst = op.fn(E)
                inst.then_inc(sem, 16)
                op.tok = (sem, 16 * (prev + 1))
                dma_n += 1
            else:
                inst = op.fn(E)
                if op.sig:
                    c = cnt[op.eng]
                    ep = c // EPOCH
                    if ep >= len(sems[op.eng]):
                        sems[op.eng].append(self.es.enter_context(
                            nc.semaphore("e_%s_%d" % (op.eng, ep))))
                    sem = sems[op.eng][ep]
                    inst.then_inc(sem, 1)
                    op.tok = (sem, c % EPOCH + 1)
                    cnt[op.eng] = c + 1
        for oid in final_wait_ops:
            wait("sp", self.ops[oid].tok)
        return dict(nops=len(self.ops), nwait=nwait, ndma=dma_n, sig=dict(cnt))


def build(nch, dbg=None):
    nc = bass.Bass("TRN2", target_bir_lowering=False, dynamic_dma_scratch_size=256)
    es = contextlib.ExitStack()
    P = Prog(nc, es)
    ntok = nch * TCH
    npos = NMETA + ntok
    nblk = nch * 4

    def din(name, shape, dt=F32):
        return nc.dram_tensor(name, list(shape), dt, kind="ExternalInput").ap()

    x_d = din("x", [ntok, D])
    meta_d = din("meta", [NMETA, D])
    wraw_d = din("wraw", [NSLOT, 128, 1024])
    gv_d = din("gv", [128, 4 * KD])
    subln_d = din("subln", [128, 1])
    pscale_d = din("pscale", [128, D])
    gfin_d = din("gfin", [128, D])
    lam_d = din("lamv", [128, 4 * 64])
    braw_d = din("braw", [128, NH * 3 * 128])
    cfar_d = din("cfar", [128, NH])
    cbf_d = din("cbf", [128, 128 * 2 + 128 + 1664], BF16)
    y_d = nc.dram_tensor("y", [ntok, D], F32, kind="ExternalOutput").ap()
    wall_d = nc.dram_tensor("wall", [NSLOT, 128, 1024], BF16, kind="Internal").ap()

    def sb(name, shape, dt):
        return es.enter_context(nc.sbuf_tensor(name, list(shape), dt))

    KT = sb("KT", [128, NH, npos], BF16)
    VC = sb("VC", [128, nblk, D], BF16)
    VM = sb("VM", [NMETA, D], BF16)
    H = sb("H", [128, 4, D], F32)
    HS = sb("HS", [128, 3, D], BF16)
    HST = sb("HST", [128, KD, TCH], BF16)
    U = sb("U", [128, KF * TCH], BF16)
    RING = sb("RING", [128, NRING, 1024], BF16)
    STG = sb("STG", [128, NSTAGE, 1024], F32)
    SIL = sb("SIL", [128, 2, TCH], F32)
    MT = sb("MT", [128, NH, 3, 128], BF16)
    CB = sb("CB", [128, 128 * 3 + 1664], BF16)
    GF = sb("GF", [128, D], F32)
    PSC = None
    GV = sb("GV", [128, 4, KD], F32)
    SM = sb("SM", [128, 64], F32)
    ONESF = sb("ONESF", [128, 128], F32)
    PS = es.enter_context(nc.psum_tensor("PS", [128, 8, 512], F32))

    ident = CB[:, 0:128]
    ones = CB[:, 128:256]
    dmask = CB[:, 256:384]
    AOFF = 384

    def a_main(g):
        return CB[:, AOFF + g * 128: AOFF + (g + 1) * 128]

    def a_halo(g):
        return CB[:, AOFF + 512 + g * 128: AOFF + 512 + (g + 1) * 128]

    def a_halometa(g):
        return CB[0:16, AOFF + 1024 + g * 128: AOFF + 1024 + (g + 1) * 128]

    def a_first(g, lo):
        o = AOFF + 1536 + (g * 2 + lo) * 16
        return CB[0:16, o:o + 16]

    ACT_T = U[:, :].rearrange("p (f t) -> p f t", t=TCH)
    QTZ = U[:, 0:2048].rearrange("p (s m t) -> p s m t", s=2, m=2)
    OT = U[:, 2048:6144].rearrange("p (a t) -> p a t", t=TCH)
    PT = U[:, 6144:8192].rearrange("p (a t) -> p a t", t=TCH)
    SQ = U[:, 8192:8704]
    TT = U[:, 8704:10752].bitcast(F32).rearrange("p (a t) -> p a t", t=TCH)
    OUTT = U[:, 0:4096].bitcast(F32).rearrange("p (a t) -> p a t", t=D)
    DACC = SIL
    BRAW = U[:, 0:6144].bitcast(F32).rearrange("p (h k t) -> p h k t", h=NH, k=3)
    PSCL = U[:, 6144:8192].bitcast(F32)
    LAMV = U[:, 8192:8704].bitcast(F32)

    SSQ = lambda j: SM[:, j:j + 1]
    LNV = lambda j: SM[:, 4 + j:5 + j]
    RSTD = lambda j: SM[:, 8 + j:9 + j]
    NEGLAM = SM[:, 12:13]
    LS = lambda j: SM[:, 13 + j:14 + j]
    CF = SM[:, 20:28]
    SUBL = SM[:, 28:29]
    EPSC = SM[:, 29:30]

    def dma(out, in_):
        return P.add("sp", lambda e: e.dma_start(out=out, in_=in_), reads=[in_], writes=[out], dma=True)

    def mm(out, lhsT, rhs, start, stop):
        P.add("pe", lambda e: e.matmul(out, lhsT, rhs, start=start, stop=stop, skip_group_check=True),
              reads=[lhsT, rhs], writes=[out])

    def act(out, in_, func, bias=None, scale=None, extra_reads=()):
        kw = {}
        if bias is not None:
            kw["bias"] = bias
        if scale is not None:
            kw["scale"] = scale
        rd = [in_] + list(extra_reads)
        if bias is not None and not isinstance(bias, (int, float)):
            rd.append(bias)
        P.add("act", lambda e: e.activation(out=out, in_=in_, func=func, **kw), reads=rd, writes=[out])

    def tt(out, in0, in1, op, eng="dve"):
        P.add(eng, lambda e: e.tensor_tensor(out=out, in0=in0, in1=in1, op=op),
              reads=[in0, in1], writes=[out])

    def ts(out, in0, s1, op0, s2=None, op1=None, eng="dve"):
        rd = [in0]
        if not isinstance(s1, (int, float)):
            rd.append(s1)
        if s2 is not None and not isinstance(s2, (int, float)):
            rd.append(s2)
        if op1 is None:
            P.add(eng, lambda e: e.tensor_scalar(out=out, in0=in0, scalar1=s1, scalar2=None, op0=op0),
                  reads=rd, writes=[out])
        else:
            P.add(eng, lambda e: e.tensor_scalar(out=out, in0=in0, scalar1=s1, scalar2=s2, op0=op0, op1=op1),
                  reads=rd, writes=[out])

    def stt(out, in0, scalar, in1, op0, op1):
        rd = [in0, in1]
        if not isinstance(scalar, (int, float)):
            rd.append(scalar)
        P.add("dve", lambda e: e.scalar_tensor_tensor(out=out, in0=in0, scalar=scalar, in1=in1, op0=op0, op1=op1),
              reads=rd, writes=[out])

    def copy(out, in_, eng):
        if eng == "act":
            P.add("act", lambda e: e.copy(out=out, in_=in_), reads=[in_], writes=[out])
        else:
            P.add(eng, lambda e: e.tensor_copy(out=out, in_=in_), reads=[in_], writes=[out])

    bank_ctr = [0]

    def next_bank():
        b = bank_ctr[0] % 8
        bank_ctr[0] += 1
        return b

    ring_ctr = [0]
    stage_of = {}
    stage_ctr = [0]
    converted = [False] * NSLOT

    def ensure_loaded(s):
        if s >= NSLOT or converted[s] or s in stage_of:
            return
        st = stage_ctr[0] % NSTAGE
        stage_ctr[0] += 1
        stage_of[s] = st
        dma(STG[:, st, :], wraw_d[s])

    def bc_last(ap2, n):
        d = [list(x) for x in ap2.ap]
        return bass.AP(ap2.tensor, ap2.offset, d + [[0, n]])

    def convert(s, st, dst):
        src = STG[:, st, :]
        if s < S_G0:
            for gl in range(2):
                g = 2 * s + gl
                for kc in range(2):
                    o = (gl * 2 + kc) * 256
                    stt(dst[:, o:o + 256], src[:, o:o + 256], GV[:, 0, 2 * g + kc:2 * g + kc + 1],
                        PSCL[:, g * 256:(g + 1) * 256], ALU.mult, ALU.mult)
            return
        if S_D0 <= s < S_K or s >= S_D1:
            copy(dst, src, "dve")
            return
        if S_V <= s < S_Q:
            k = s - S_V
            ts(dst, src, GV[:, 2, k:k + 1], ALU.mult)
            return
        if S_O <= s < S_G1:
            ts(dst, src, SUBL, ALU.mult, 1.0 - LAMBDA_INIT, ALU.mult)
            return
        if s < S_D0:
            a = 1
        elif s < S_O:
            a = 2
        else:
            a = 3
        d3 = dst.rearrange("p (k j) -> p k j", j=128)
        s3 = src.rearrange("p (k j) -> p k j", j=128)
        tt(d3, s3, bc_last(GV[:, a, :], 128), ALU.mult)

    def get_slot(s):
        r = ring_ctr[0] % NRING
        ring_ctr[0] += 1
        dst = RING[:, r, :]
        if not converted[s]:
            ensure_loaded(s)
            ensure_loaded(s + 1)
            convert(s, stage_of.pop(s), dst)
            dma(wall_d[s], dst)
            converted[s] = True
        else:
            dma(dst, wall_d[s])
        return dst

    dma(CB[:, :], cbf_d)
    dma(GV[:, :, :].rearrange("p a k -> p (a k)"), gv_d)
    dma(SUBL, subln_d)
    dma(GF[:, :], gfin_d)
    dma(CF, cfar_d)
    dma(PSCL, pscale_d)
    dma(LAMV, lam_d)
    dma(BRAW.rearrange("p h k t -> p (h k t)"), braw_d)
    P.add("dve", lambda e: e.memset(EPSC, EPS), writes=[EPSC])
    P.add("pool", lambda e: e.memset(ONESF[:, :], 1.0), writes=[ONESF[:, :]])
    for j in range(2):
        a0 = LAMV[:, (2 * j) * 64:(2 * j + 1) * 64]
        a1 = LAMV[:, (2 * j + 1) * 64:(2 * j + 2) * 64]
        P.add("dve", lambda e, a0=a0, a1=a1, j=j: e.scalar_tensor_tensor(
            out=HS[:, 0, 0:64], in0=a0, scalar=1.0, in1=a1,
            op0=ALU.mult, op1=ALU.mult, accum_out=LS(j)), reads=[a0, a1], writes=[LS(j), HS[:, 0, 0:64]])
        act(LS(2 + j), LS(j), AF.Exp)
    stt(NEGLAM, LS(3), -LAMBDA_INIT, LS(2), ALU.add, ALU.subtract)
    for h in range(NH):
        for k in range(3):
            ts(BRAW[:, h, k, :], BRAW[:, h, k, :], CF[:, h:h + 1], ALU.subtract)
            act(MT[:, h, k, :], BRAW[:, h, k, :], AF.Exp)
        tt(MT[:, h, 0, :], MT[:, h, 0, :], dmask, ALU.mult)

    def norm_tile(i, R, dst):
        hin = H[0:R, i, :]
        P.add("act", lambda e: e.activation(
            out=dst, in_=hin, func=AF.Square, accum_out=SM[0:R, i:i + 1]),
            reads=[hin], writes=[SM[0:R, i:i + 1], dst])
        act(SM[0:R, 4 + i:5 + i], SM[0:R, i:i + 1], AF.Ln, bias=EPSC[0:R, :], scale=1.0 / D)
        act(SM[0:R, 8 + i:9 + i], SM[0:R, 4 + i:5 + i], AF.Exp, scale=-0.5)
        ts(dst, hin, SM[0:R, 8 + i:9 + i], ALU.mult)

    hs_ctr = [0]

    def norm_to_hsT(tiles):
        for i, R in tiles:
            slot = hs_ctr[0] % 2
            hs_ctr[0] += 1
            hs = HS[0:R, slot, :]
            norm_tile(i, R, hs)
            for half in range(2):
                b = next_bank()
                for q in range(4):
                    kc = half * 4 + q
                    mm(PS[:, b, q * 128:q * 128 + R], hs[:, kc * 128:(kc + 1) * 128], ident[0:R, 0:R], True, True)
                src = PS[:, b, :].rearrange("p (q t) -> p q t", t=128)[:, :, 0:R]
                copy(HST[:, half * 4:half * 4 + 4, i * 128:i * 128 + R], src, "dve")

    def acc8(tiles, nk, lhs_fn, slot_fn, evac_fn):
        for k in range(nk):
            w = slot_fn(k)
            for ti, (i, R) in enumerate(tiles):
                for n in range(2):
                    mm(PS[0:R, ti * 2 + n, :], lhs_fn(k, i, R), w[:, n * 512:(n + 1) * 512], k == 0, k == nk - 1)
        for ti, (i, R) in enumerate(tiles):
            for n in range(2):
                evac_fn(i, R, n, PS[0:R, ti * 2 + n, :])

    def resid_add(i, R, n, ps):
        hv = H[0:R, i, n * 512:(n + 1) * 512]
        tt(hv, ps, hv, ALU.add)

    def ffn(layer, tiles, T):
        norm_to_hsT(tiles)
        sg = S_G0 if layer == 0 else S_G1
        sd = S_D0 if layer == 0 else S_D1
        for f in range(KF):
            wg = get_slot(sg + 2 * f)
            wu = get_slot(sg + 2 * f + 1)
            bg = next_bank()
            bu = next_bank()
            for k in range(KD):
                mm(PS[:, bg, 0:T], wg[:, k * 128:(k + 1) * 128], HST[:, k, 0:T], k == 0, k == KD - 1)
            for k in range(KD):
                mm(PS[:, bu, 0:T], wu[:, k * 128:(k + 1) * 128], HST[:, k, 0:T], k == 0, k == KD - 1)
            st = SIL[:, f % 2, 0:T]
            act(st, PS[:, bg, 0:T], AF.Silu)
            tt(ACT_T[:, f, 0:T], st, PS[:, bu, 0:T], ALU.mult)
        acc8(tiles, KF,
             lambda k, i, R: ACT_T[:, k, i * 128:i * 128 + R],
             lambda k: get_slot(sd + k), resid_add)

    def pool_layer(tiles, is_meta, first_chunk):
        wp = [get_slot(S_POOL), get_slot(S_POOL + 1)]
        for (i, R) in tiles:
            slot = hs_ctr[0] % 2
            hs_ctr[0] += 1
            hs = HS[0:R, slot, :]
            norm_tile(i, R, hs)
            if is_meta:
                halo = None
            elif i == 0:
                halo = HS[0:16, 2, :] if first_chunk else HS[:, 2, :]
            else:
                halo = HS[:, (slot + 1) % 2, :]
            for half in range(2):
                b = next_bank()
                for q in range(4):
                    kc = half * 4 + q
                    g = kc // 2
                    o = PS[:, b, q * 128:q * 128 + R]
                    lhs = hs[:, kc * 128:(kc + 1) * 128]
                    if is_meta:
                        mm(o, lhs, a_first(g, 0), True, False)
                        mm(o, lhs, a_first(g, 1), False, True)
                    else:
                        mm(o, lhs, a_main(g), True, False)
                        if i == 0 and first_chunk:
                            mm(o, halo[:, kc * 128:(kc + 1) * 128], a_halometa(g), False, True)
                        else:
                            mm(o, halo[:, kc * 128:(kc + 1) * 128], a_halo(g), False, True)
                src = PS[:, b, :].rearrange("p (q t) -> p q t", t=128)[:, :, 0:R]
                copy(HST[:, half * 4:half * 4 + 4, i * 128:i * 128 + R], src, "dve")
            if is_meta:
                copy(HS[0:16, 2, :], hs, "pool")
            elif i == 3:
                copy(HS[:, 2, :], hs, "pool")
            b0 = next_bank()
            b1 = next_bank()
            for g in range(4):
                bb = b0 if g < 2 else b1
                for kk in range(2):
                    w = wp[g // 2][:, ((g % 2) * 2 + kk) * 256:((g % 2) * 2 + kk + 1) * 256]
                    mm(PS[0:R, bb, (g % 2) * 256:(g % 2 + 1) * 256],
                       HST[:, 2 * g + kk, i * 128:i * 128 + R], w, kk == 0, kk == 1)
            resid_add(i, R, 0, PS[0:R, b0, :])
            resid_add(i, R, 1, PS[0:R, b1, :])

    def kv_proj(tiles, T, pos0, blk0):
        norm_to_hsT(tiles)
        for h in range(NH):
            w = get_slot(S_K + h)
            b = next_bank()
            for k in range(KD):
                mm(PS[:, b, 0:T], w[:, k * 128:(k + 1) * 128], HST[:, k, 0:T], k == 0, k == KD - 1)
            copy(KT[:, h, pos0:pos0 + T], PS[:, b, 0:T], "act")

        def v_evac(i, R, n, ps):
            if blk0 is None:
                copy(VM[0:R, n * 512:(n + 1) * 512], ps, "dve")
            else:
                copy(VC[:, blk0 + i, n * 512:(n + 1) * 512], ps, "dve")
        acc8(tiles, KD, lambda k, i, R: HST[:, k, i * 128:i * 128 + R],
             lambda k: get_slot(S_V + k), v_evac)

    s_ctr = [0]

    def attention(c):
        pos0 = NMETA + c * TCH
        pend = [None]
        for qs in range(2):
            P.add("pool", lambda e, qs=qs: e.memset(QTZ[64:128, qs, 0, :], 0.0), writes=[QTZ[64:128, qs, 0, :]])
            P.add("pool", lambda e, qs=qs: e.memset(QTZ[0:64, qs, 1, :], 0.0), writes=[QTZ[0:64, qs, 1, :]])

        def emit_ssq(h):
            b = s_ctr[0] % 4
            s_ctr[0] += 1
            mm(PS[:, b, :], ones, SQ, True, True)
            act(TT[:, 1, :], PS[:, b, :], AF.Ln, bias=EPSC, scale=1.0 / 128)
            act(TT[:, 1, :], TT[:, 1, :], AF.Exp, scale=-0.5)
            tt(OT[:, h, :], TT[:, 0, :], TT[:, 1, :], ALU.mult)

        for h in range(NH):
            w = get_slot(S_Q + h)
            b = s_ctr[0] % 4
            s_ctr[0] += 1
            for k in range(KD):
                mm(PS[:, b, :], w[:, k * 128:(k + 1) * 128], HST[:, k, :], k == 0, k == KD - 1)
            qs = h % 2
            copy(QTZ[0:64, qs, 0, :], PS[0:64, b, :], "dve")
            copy(QTZ[64:128, qs, 1, :], PS[64:128, b, :], "dve")
            if c == 0:
                blocks = [("d", 0), ("d", 3), ("d", 2), ("d", 1), ("m", 0)]
            else:
                blocks = ([("f", 0), ("m", 0)] + [("f", j) for j in range(1, 4 * c)]
                          + [("d", jj) for jj in (3, 2, 1, 0)])
            items = [(blk, m) for blk in blocks for m in (0, 1)]
            n = len(items)
            sb_of = {}

            def qk(ix):
                (kind, j), m = items[ix]
                b = s_ctr[0] % 4
                s_ctr[0] += 1
                sb_of[ix] = b
                if kind == "m":
                    mm(PS[0:16, b, :], KT[:, h, 0:16], QTZ[:, qs, m, :], True, True)
                elif kind == "f":
                    k0 = NMETA + j * 128
                    mm(PS[:, b, :], KT[:, h, k0:k0 + 128], QTZ[:, qs, m, :], True, True)
                else:
                    k0 = pos0 + j * 128
                    mm(PS[:, b, j * 128:512], KT[:, h, k0:k0 + 128], QTZ[:, qs, m, j * 128:512], True, True)

            LA = 2
            for ix in range(min(LA, n)):
                qk(ix)
            for ix in range(n):
                if ix + LA < n:
                    qk(ix + LA)
                (kind, j), m = items[ix]
                b = sb_of[ix]
                pb = ix % 4
                first = ix < 2
                last = ix >= n - 2
                if kind == "m":
                    KR, q0 = 16, 0
                elif kind == "f":
                    KR, q0 = 128, 0
                else:
                    KR, q0 = 128, j * 128
                pt = PT[0:KR, pb, q0:512]
                act(pt, PS[0:KR, b, q0:512], AF.Exp, scale=0.125)
                if kind == "m" and c == 0:
                    tt(PT[0:16, pb, 0:128], PT[0:16, pb, 0:128], MT[0:16, h, 2, :], ALU.mult, eng=FIX_ENG)
                if kind == "f" and j == 4 * c - 1:
                    tt(PT[:, pb, 0:128], PT[:, pb, 0:128], MT[:, h, 1, :], ALU.mult, eng=FIX_ENG)
                if kind == "d":
                    tt(PT[:, pb, q0:q0 + 128], PT[:, pb, q0:q0 + 128], MT[:, h, 0, :], ALU.mult, eng=FIX_ENG)
                    if j < 3:
                        tt(PT[:, pb, q0 + 128:q0 + 256], PT[:, pb, q0 + 128:q0 + 256], MT[:, h, 1, :], ALU.mult,
                           eng=FIX_ENG)
                if first:
                    copy(DACC[:, m, :], pt, "dve")
                else:
                    tt(DACC[0:KR, m, q0:512], DACC[0:KR, m, q0:512], pt, ALU.add)
                if kind == "m":
                    vv = VM[0:16, h * 128:(h + 1) * 128]
                elif kind == "f":
                    vv = VC[:, j, h * 128:(h + 1) * 128]
                else:
                    vv = VC[:, 4 * c + j, h * 128:(h + 1) * 128]
                mm(PS[:, 4 + m, q0:512], vv, pt, first, last)
                if ix == 1 and pend[0] is not None:
                    emit_ssq(pend[0])
                    pend[0] = None
            if pend[0] is not None:
                emit_ssq(pend[0])
                pend[0] = None
            for m in range(2):
                mm(PS[:, 6 + m, :], ONESF[:, :], DACC[:, m, :], True, True)
            for m in range(2):
                act(TT[:, m, :], PS[:, 6 + m, :], AF.Ln)
                act(TT[:, m, :], TT[:, m, :], AF.Exp, scale=-1.0)
                tt(TT[:, m, :], PS[:, 4 + m, :], TT[:, m, :], ALU.mult)
            stt(TT[:, 0, :], TT[:, 1, :], NEGLAM, TT[:, 0, :], ALU.mult, ALU.add)
            act(SQ, TT[:, 0, :], AF.Square)
            pend[0] = h
            if not DEFER_SSQ:
                emit_ssq(h)
                pend[0] = None
        if pend[0] is not None:
            emit_ssq(pend[0])

    def wo_proj(tiles):
        acc8(tiles, NH, lambda k, i, R: OT[:, k, i * 128:i * 128 + R],
             lambda k: get_slot(S_O + k), resid_add)

    out_ops = []

    def final_out(c, tiles):
        for (i, R) in tiles:
            hin = H[0:R, i, :]
            jk = HS[:, hs_ctr[0] % 2, :]
            hs_ctr[0] += 1
            P.add("act", lambda e, hin=hin, i=i, jk=jk: e.activation(
                out=jk, in_=hin, func=AF.Square, accum_out=SM[:, i:i + 1]),
                reads=[hin], writes=[SM[:, i:i + 1], jk])
            act(SM[:, 4 + i:5 + i], SM[:, i:i + 1], AF.Ln, bias=EPSC, scale=1.0 / D)
            act(SM[:, 8 + i:9 + i], SM[:, 4 + i:5 + i], AF.Exp, scale=-0.5)
            o = OUTT[:, i % 2, :]
            stt(o, hin, SM[:, 8 + i:9 + i], GF[:, :], ALU.mult, ALU.mult)
            r0 = c * TCH + i * 128
            out_ops.append(dma(y_d[r0:r0 + 128, :], o))

    mt = [(0, NMETA)]
    dma(H[0:NMETA, 0, :], meta_d)
    pool_layer(mt, True, False)
    ffn(0, mt, NMETA)
    kv_proj(mt, NMETA, 0, None)

    tiles = [(i, 128) for i in range(4)]
    for c in range(nch):
        for i in range(4):
            r0 = c * TCH + i * 128
            dma(H[:, i, :], x_d[r0:r0 + 128, :])
        pool_layer(tiles, False, c == 0)
        ffn(0, tiles, TCH)
        kv_proj(tiles, TCH, NMETA + c * TCH, 4 * c)
        def dump():
            for i in range(4):
                r0 = c * TCH + i * 128
                out_ops.append(dma(y_d[r0:r0 + 128, :], H[:, i, :]))
        if dbg == "ffn0":
            dump()
            continue
        if dbg == "vc":
            for i in range(4):
                copy(OUTT[:, i % 2, :], VC[:, 4 * c + i, :], "dve")
                out_ops.append(dma(y_d[c * TCH + i * 128:c * TCH + (i + 1) * 128, :], OUTT[:, i % 2, :]))
            continue
        if dbg == "kt":
            for hh in range(4):
                copy(OUTT[:, hh % 2, 0:512], KT[:, hh, NMETA + c * TCH:NMETA + (c + 1) * TCH], "dve")
                out_ops.append(dma(y_d[c * TCH + hh * 128:c * TCH + (hh + 1) * 128, 0:512], OUTT[:, hh % 2, 0:512]))
            continue
        attention(c)
        if dbg == "ot":
            for i in range(4):
                for hh in range(NH):
                    copy(STG[:, i % 2, hh * 128:(hh + 1) * 128], OT[:, hh, i * 128:(i + 1) * 128], "dve")
                out_ops.append(dma(y_d[c * TCH + i * 128:c * TCH + (i + 1) * 128, :], STG[:, i % 2, :]))
            continue
        wo_proj(tiles)
        if dbg == "wo":
            dump()
            continue
        ffn(1, tiles, TCH)
        final_out(c, tiles)

    stats = P.emit(out_ops)
    es.close()
    return nc, stats


def _t5_bucket(rel):
    n = np.maximum(rel, 0)
    max_exact = 16
    nf = np.maximum(n, max_exact).astype(np.float32)
    large = max_exact + (np.log(nf / max_exact) / math.log(128 / max_exact) * (32 - max_exact)).astype(np.int32)
    large = np.minimum(large, 31)
    return np.where(n < max_exact, n, large)


def _const_bf16():
    c = np.zeros((128, 128 * 3 + 1664), np.float32)
    c[:, 0:128] = np.eye(128)
    c[:, 128:256] = 1.0
    kl = np.arange(128)[:, None]
    ql = np.arange(128)[None, :]
    c[:, 256:384] = (ql >= kl)
    A = 384
    for g, w in enumerate((2, 4, 8, 16)):
        tp = np.arange(128)[:, None]
        t = np.arange(128)[None, :]
        main = ((tp <= t) & (tp >= t - w + 1)) * (1.0 / w) - (tp == t) * 1.0
        c[:, A + g * 128:A + (g + 1) * 128] = main
        halo = ((tp - 128) >= (t - w + 1)) * (1.0 / w)
        c[:, A + 512 + g * 128:A + 512 + (g + 1) * 128] = halo
        tpm = np.arange(16)[:, None]
        hm = ((tpm - 16) >= (t - w + 1)) * (1.0 / w)
        c[0:16, A + 1024 + g * 128:A + 1024 + (g + 1) * 128] = hm
        t16 = np.arange(16)[None, :]
        cnt = np.minimum(t16 + 1, w).astype(np.float64)
        first = ((tpm <= t16) & (tpm >= t16 - w + 1)) / cnt - (tpm == t16) * 1.0
        hi = first.astype(np.float32).astype(ml_dtypes.bfloat16).astype(np.float32)
        lo = (first - hi).astype(np.float32)
        o = A + 1536 + (g * 2) * 16
        c[0:16, o:o + 16] = hi
        c[0:16, o + 16:o + 32] = lo
    return c.astype(ml_dtypes.bfloat16)


def _weight_slots(inp):
    S = np.empty((NSLOT, 128, 1024), np.float32)
    pw = np.asarray(inp["pool_w"], np.float32)[0]
    for sl in range(2):
        blk = pw[2 * sl:2 * sl + 2].reshape(2, 2, 128, 256).transpose(2, 0, 1, 3)
        S[S_POOL + sl] = blk.reshape(128, 1024)

    def colk(w, ncol):
        return w.reshape(KD, 128, ncol, 128).transpose(2, 1, 0, 3).reshape(ncol, 128, 1024)

    for l, (sg, sd) in enumerate(((S_G0, S_D0), (S_G1, S_D1))):
        wg = colk(np.asarray(inp["ffn_w_gate"], np.float32)[l], KF)
        wu = colk(np.asarray(inp["ffn_w_up"], np.float32)[l], KF)
        S[sg:sg + 2 * KF:2] = wg
        S[sg + 1:sg + 2 * KF:2] = wu
        S[sd:sd + KF] = np.asarray(inp["ffn_w_down"], np.float32)[l].reshape(KF, 128, 1024)
    wqkv = np.asarray(inp["attn_w_qkv"], np.float32)[0]
    S[S_Q:S_Q + NH] = colk(wqkv[:, 0:1024], NH)
    S[S_K:S_K + NH] = colk(wqkv[:, 1024:2048], NH)
    S[S_V:S_V + KD] = wqkv[:, 2048:3072].reshape(KD, 128, 1024)
    S[S_O:S_O + NH] = np.asarray(inp["attn_w_o"], np.float32)[0].reshape(NH, 128, 1024)
    return S


def _host_inputs(inp, nch):
    f32 = lambda a: np.ascontiguousarray(np.asarray(a, np.float32))
    rb = f32(inp["rel_bias"])
    kl = np.arange(128)[:, None]
    ql = np.arange(128)[None, :]
    braw = np.zeros((128, NH, 3, 128), np.float32)
    bd = _t5_bucket(ql - kl)
    bs = _t5_bucket(128 + ql - kl)
    bm = _t5_bucket(16 + ql - kl)
    for h in range(NH):
        braw[:, h, 0, :] = rb[bd, h]
        braw[:, h, 1, :] = rb[bs, h]
        braw[:, h, 2, :] = rb[bm, h]
    gv = np.stack([f32(inp["mix_norm_g"])[0], f32(inp["ffn_norm_g"])[0],
                   f32(inp["mix_norm_g"])[1], f32(inp["ffn_norm_g"])[1]], 0)
    gv = gv.reshape(4, KD, 128).transpose(2, 0, 1).reshape(128, 4 * KD)
    lamv = np.concatenate([f32(inp["lambda_q1"])[0], f32(inp["lambda_k1"])[0],
                           f32(inp["lambda_q2"])[0], f32(inp["lambda_k2"])[0]])
    shared = {
        "meta": f32(inp["meta_tokens"]),
        "wraw": _weight_slots(inp),
        "gv": np.ascontiguousarray(gv),
        "subln": f32(inp["subln_g"])[0].reshape(128, 1).copy(),
        "pscale": np.ascontiguousarray(np.broadcast_to(f32(inp["pool_scale"])[0][None, :], (128, D))),
        "gfin": np.ascontiguousarray(np.broadcast_to(f32(inp["final_norm_g"])[None, :], (128, D))),
        "lamv": np.ascontiguousarray(np.broadcast_to(lamv[None, :], (128, 256))),
        "braw": braw.reshape(128, NH * 3 * 128),
        "cfar": np.ascontiguousarray(np.broadcast_to(rb[31][None, :], (128, NH))),
        "cbf": _const_bf16(),
    }
    return shared


_CACHE = {}
TRACE = False
LAST_EXEC_NS = None


def run(inp, nch, ncores):
    if nch not in _CACHE:
        _CACHE[nch] = build(nch)
    nc, stats = _CACHE[nch]
    shared = _host_inputs(inp, nch)
    x = np.asarray(inp["x"], np.float32)
    in_maps = []
    for b in range(ncores):
        m = dict(shared)
        m["x"] = np.ascontiguousarray(x[b, :nch * TCH, :])
        in_maps.append(m)
    if TRACE:
        res = run_bass_kernel_spmd(nc, in_maps, core_ids=list(range(ncores)), trace=True)
        global LAST_EXEC_NS
        LAST_EXEC_NS = res.exec_time_ns
    else:
        res = run_bass_kernel_spmd(nc, in_maps, core_ids=list(range(ncores)))
    return np.stack([np.asarray(r["y"]) for r in res.results], 0)


def kernel(**inputs):
    out = run(inputs, SEQ // TCH, 8)
    return out.astype(np.float32)
```
